# Optimizing a Trainium2 kernel written in Bass

```python
import jax, jax.numpy as jnp
from jax import lax
import numpy as np

D_MODEL = 4096
BATCH = 2
SEQ = 4096
DEPTH = 2
DEC_BATCH = 32
DEC_SEQ = 64
PAST_LEN = 4096

CHUNK = 64
N_EVEN = (DEPTH + 1) // 2
N_ODD = DEPTH // 2
EPS = 1e-6

A_WIDTH = D_MODEL // 2
HEAD_DIM = 128
N_HEADS_A = A_WIDTH // HEAD_DIM
Q_BLOCK = 128
ATTN_SCALE = HEAD_DIM ** -0.5
B_WIDTH = D_MODEL // 2
POOL_WINDOWS = (2, 4, 8, 16)
N_POOL_GROUPS = len(POOL_WINDOWS)
POOL_GROUP = B_WIDTH // N_POOL_GROUPS
POOL_HIST = max(POOL_WINDOWS) - 1
GATE0_WIDTH = A_WIDTH + B_WIDTH
SPLIT0 = [A_WIDTH, 2 * A_WIDTH, 3 * A_WIDTH, 3 * A_WIDTH + N_HEADS_A, 3 * A_WIDTH + N_HEADS_A + B_WIDTH]
IN0_WIDTH = 3 * A_WIDTH + N_HEADS_A + B_WIDTH + GATE0_WIDTH
C_WIDTH = D_MODEL
SGU_CHUNK = 128
N_SGU_GROUPS = 16
SGU_GROUP = C_WIDTH // N_SGU_GROUPS
IN1_WIDTH = 3 * C_WIDTH

kernel_name = 'fox_pool_sgu_stream_step'


def rms_norm(x, g):
    xf = x.astype(jnp.float32)
    y = xf * lax.rsqrt(jnp.mean(xf * xf, axis=-1, keepdims=True) + EPS)
    return (y * g.astype(jnp.float32)).astype(x.dtype)


def layer_norm(x, g, b):
    xf = x.astype(jnp.float32)
    xc = xf - jnp.mean(xf, axis=-1, keepdims=True)
    var = jnp.mean(xc * xc, axis=-1, keepdims=True)
    return (xc * lax.rsqrt(var + EPS) * g.astype(jnp.float32) + b.astype(jnp.float32)).astype(x.dtype)


def modulate(x, c, w_ada, b_ada, g_norm):
    mod = jax.nn.silu(c) @ w_ada + b_ada
    shift, scale, gate = jnp.split(mod[:, None, :], 3, axis=-1)
    h = rms_norm(x, g_norm) * (1 + scale) + shift
    return h, gate


def fox_prompt(q, k, v, logf):
    bsz, seq, nh, hd = q.shape
    nb = seq // Q_BLOCK
    ft = jnp.cumsum(logf, axis=1).transpose(0, 2, 1)
    kpos = jnp.arange(seq)
    qb = q.reshape(bsz, nb, Q_BLOCK, nh, hd).transpose(1, 0, 2, 3, 4)
    fb = ft.reshape(bsz, nh, nb, Q_BLOCK).transpose(2, 0, 1, 3)
    pb = kpos.reshape(nb, Q_BLOCK)

    def block(args):
        qi, fi, pi = args
        s = jnp.einsum('bqhd,bkhd->bhqk', qi, k, preferred_element_type=jnp.float32) * ATTN_SCALE
        s = s + fi[..., :, None] - ft[:, :, None, :]
        s = jnp.where(kpos[None, None, None, :] <= pi[None, None, :, None], s, -jnp.inf)
        p = jax.nn.softmax(s, axis=-1)
        return jnp.einsum('bhqk,bkhd->bqhd', p.astype(v.dtype), v)

    o = lax.map(block, (qb, fb, pb))
    return o.transpose(1, 0, 2, 3, 4).reshape(bsz, seq, nh * hd)


def fox_sample(q, k, v, logf, ck, cv, clogf):
    bsz, L, nh, hd = q.shape
    fn = jnp.cumsum(logf, axis=1).transpose(0, 2, 1)
    clf = clogf.astype(jnp.float32)
    g_past = (lax.cumsum(clf, axis=1, reverse=True) - clf).transpose(0, 2, 1)
    s_past = jnp.einsum('bqhd,bkhd->bhqk', q, ck, preferred_element_type=jnp.float32) * ATTN_SCALE
    s_past = s_past + fn[..., :, None] + g_past[:, :, None, :]
    s_new = jnp.einsum('bqhd,bkhd->bhqk', q, k, preferred_element_type=jnp.float32) * ATTN_SCALE
    s_new = s_new + fn[..., :, None] - fn[:, :, None, :]
    causal = jnp.tril(jnp.ones((L, L), dtype=bool))
    s_new = jnp.where(causal[None, None], s_new, -jnp.inf)
    m = jnp.maximum(s_past.max(-1, keepdims=True), s_new.max(-1, keepdims=True))
    p_past = jnp.exp(s_past - m)
    p_new = jnp.exp(s_new - m)
    denom = p_past.sum(-1, keepdims=True) + p_new.sum(-1, keepdims=True)
    o = (jnp.einsum('bhqk,bkhd->bhqd', p_past.astype(cv.dtype), cv, preferred_element_type=jnp.float32)
         + jnp.einsum('bhqk,bkhd->bhqd', p_new.astype(v.dtype), v, preferred_element_type=jnp.float32)) / denom
    return o.transpose(0, 2, 1, 3).reshape(bsz, L, nh * hd).astype(v.dtype)


def pool_mix(u, hist, pos0, w_pool, ls_pool):
    bsz, L, _ = u.shape
    ext = jnp.concatenate([hist.astype(u.dtype), u], axis=1)
    cs = jnp.cumsum(ext.astype(jnp.float32), axis=1)
    cs = jnp.concatenate([jnp.zeros_like(cs[:, :1]), cs], axis=1)
    pos = pos0 + jnp.arange(L)
    means = []
    for gi, w in enumerate(POOL_WINDOWS):
        sl = slice(gi * POOL_GROUP, (gi + 1) * POOL_GROUP)
        win = cs[:, POOL_HIST + 1:POOL_HIST + 1 + L, sl] - cs[:, POOL_HIST + 1 - w:POOL_HIST + 1 - w + L, sl]
        cnt = jnp.minimum(w, pos + 1).astype(jnp.float32)
        means.append(win / cnt[None, :, None])
    pooled = jnp.stack(means, axis=2)
    d = pooled - u.reshape(bsz, L, N_POOL_GROUPS, POOL_GROUP).astype(jnp.float32)
    y = jnp.einsum('blgc,gce->blge', d.astype(u.dtype), w_pool).reshape(bsz, L, B_WIDTH) * ls_pool
    return y, ext[:, -POOL_HIST:]


def layer_ab(h, w_in, b_f, g_q, g_k, w_pool, ls_pool, w_out, pool_hist, pos0, kv_cache):
    bsz, L, _ = h.shape
    z = h @ w_in
    q, k, v, f, u_b, gate = jnp.split(z, SPLIT0, axis=-1)
    q = rms_norm(q.reshape(bsz, L, N_HEADS_A, HEAD_DIM), g_q)
    k = rms_norm(k.reshape(bsz, L, N_HEADS_A, HEAD_DIM), g_k)
    v = v.reshape(bsz, L, N_HEADS_A, HEAD_DIM)
    logf = jax.nn.log_sigmoid(f.astype(jnp.float32) + b_f.astype(jnp.float32))
    if kv_cache is None:
        a = fox_prompt(q, k, v, logf)
    else:
        a = fox_sample(q, k, v, logf, kv_cache[0], kv_cache[1], kv_cache[2])
    b, new_hist = pool_mix(u_b, pool_hist, pos0, w_pool, ls_pool)
    mixed = jnp.concatenate([a, b.astype(a.dtype)], axis=-1) * jax.nn.silu(gate)
    return mixed @ w_out, k, v, logf, new_hist


def layer_c(h, w_in, g_v, b_v, w_s, b_s, w_out):
    bsz, L, _ = h.shape
    z = h @ w_in
    uv = jax.nn.gelu(z[..., :2 * C_WIDTH], approximate=False)
    gate = z[..., 2 * C_WIDTH:]
    u, v = uv[..., :C_WIDTH], uv[..., C_WIDTH:]
    v = layer_norm(v, g_v, b_v)
    cl = min(L, SGU_CHUNK)
    vc = v.reshape(bsz, L // cl, cl, N_SGU_GROUPS, SGU_GROUP)
    mask = jnp.tril(jnp.ones((cl, cl), dtype=bool))
    ws = jnp.where(mask[None], w_s[:, :cl, :cl], 0)
    sv = jnp.einsum('gts,bnsgd->bntgd', ws, vc) + b_s[:, :cl].T[None, None, :, :, None]
    y = (u * sv.reshape(bsz, L, C_WIDTH) * jax.nn.silu(gate)) @ w_out
    return y, v


def setup_inputs(seed: int = 0) -> dict:
    key = jax.random.key(seed)
    ks = jax.random.split(key, 26)

    def nrm(k, shape, s):
        return jax.random.normal(k, shape, jnp.float32) * s

    return {
        'x_prompt': nrm(ks[0], (BATCH, SEQ, D_MODEL), 1.0),
        'x_sample': nrm(ks[1], (DEC_BATCH, DEC_SEQ, D_MODEL), 1.0),
        'cache_k': nrm(ks[2], (N_EVEN, DEC_BATCH, PAST_LEN, N_HEADS_A, HEAD_DIM), 1.0),
        'cache_v': nrm(ks[3], (N_EVEN, DEC_BATCH, PAST_LEN, N_HEADS_A, HEAD_DIM), 1.0),
        'cache_logf': jax.nn.log_sigmoid(2.5 + nrm(ks[4], (N_EVEN, DEC_BATCH, PAST_LEN, N_HEADS_A), 1.0)),
        'state_pool': nrm(ks[5], (N_EVEN, DEC_BATCH, POOL_HIST, B_WIDTH), 1.0),
        'c_prompt': nrm(ks[6], (BATCH, D_MODEL), 1.0),
        'c_sample': nrm(ks[7], (DEC_BATCH, D_MODEL), 1.0),
        'w_ada': nrm(ks[8], (DEPTH, D_MODEL, 3 * D_MODEL), 0.5 * D_MODEL ** -0.5),
        'b_ada': nrm(ks[9], (DEPTH, 3 * D_MODEL), 0.02),
        'g_norm': 1.0 + nrm(ks[10], (DEPTH, D_MODEL), 0.1),
        'w_in_ab': nrm(ks[11], (N_EVEN, D_MODEL, IN0_WIDTH), D_MODEL ** -0.5),
        'b_forget': jax.random.uniform(ks[12], (N_EVEN, N_HEADS_A), jnp.float32, 1.0, 4.0),
        'g_q': 1.0 + nrm(ks[13], (N_EVEN, HEAD_DIM), 0.1),
        'g_k': 1.0 + nrm(ks[14], (N_EVEN, HEAD_DIM), 0.1),
        'w_pool': nrm(ks[15], (N_EVEN, N_POOL_GROUPS, POOL_GROUP, POOL_GROUP), POOL_GROUP ** -0.5),
        'ls_pool': 1.0 + nrm(ks[16], (N_EVEN, B_WIDTH), 0.1),
        'w_out_ab': nrm(ks[17], (N_EVEN, A_WIDTH + B_WIDTH, D_MODEL), (A_WIDTH + B_WIDTH) ** -0.5),
        'w_in_c': nrm(ks[18], (N_ODD, D_MODEL, IN1_WIDTH), D_MODEL ** -0.5),
        'g_v': 1.0 + nrm(ks[19], (N_ODD, C_WIDTH), 0.1),
        'b_v': nrm(ks[20], (N_ODD, C_WIDTH), 0.02),
        'w_s': nrm(ks[21], (N_ODD, N_SGU_GROUPS, SGU_CHUNK, SGU_CHUNK), SGU_CHUNK ** -0.5),
        'b_s': 1.0 + nrm(ks[22], (N_ODD, N_SGU_GROUPS, SGU_CHUNK), 0.1),
        'w_out_c': nrm(ks[23], (N_ODD, C_WIDTH, D_MODEL), C_WIDTH ** -0.5),
    }


def reference(x_prompt, x_sample, cache_k, cache_v, cache_logf, state_pool, c_prompt, c_sample,
              w_ada, b_ada, g_norm, w_in_ab, b_forget, g_q, g_k, w_pool, ls_pool, w_out_ab,
              w_in_c, g_v, b_v, w_s, b_s, w_out_c):
    yp, ys = x_prompt, x_sample
    kp_l, vp_l, lfp_l, pp_l = [], [], [], []
    ks_l, vs_l, lfs_l, ps_l, cs_l = [], [], [], [], []
    for layer in range(DEPTH):
        hp, gp = modulate(yp, c_prompt, w_ada[layer], b_ada[layer], g_norm[layer])
        hs, gs = modulate(ys, c_sample, w_ada[layer], b_ada[layer], g_norm[layer])
        i = layer // 2
        if layer % 2 == 0:
            zero_hist = jnp.zeros((yp.shape[0], POOL_HIST, B_WIDTH), yp.dtype)
            op, kp, vp, lfp, pp = layer_ab(hp, w_in_ab[i], b_forget[i], g_q[i], g_k[i], w_pool[i], ls_pool[i],
                                          w_out_ab[i], zero_hist, 0, None)
            os_, ks_, vs_, lfs, ps = layer_ab(hs, w_in_ab[i], b_forget[i], g_q[i], g_k[i], w_pool[i], ls_pool[i],
                                             w_out_ab[i], state_pool[i], PAST_LEN,
                                             (cache_k[i], cache_v[i], cache_logf[i]))
            kp_l.append(kp); vp_l.append(vp); lfp_l.append(lfp); pp_l.append(pp)
            ks_l.append(ks_); vs_l.append(vs_); lfs_l.append(lfs); ps_l.append(ps)
        else:
            op, _ = layer_c(hp, w_in_c[i], g_v[i], b_v[i], w_s[i], b_s[i], w_out_c[i])
            os_, v_c = layer_c(hs, w_in_c[i], g_v[i], b_v[i], w_s[i], b_s[i], w_out_c[i])
            cs_l.append(v_c)
        yp = yp + gp * op
        ys = ys + gs * os_
    k_prompt = jnp.stack(kp_l)
    v_prompt = jnp.stack(vp_l)
    logf_prompt = jnp.stack(lfp_l)
    pool_prompt = jnp.stack(pp_l)
    k_sample = jnp.stack(ks_l)
    v_sample = jnp.stack(vs_l)
    logf_sample = jnp.stack(lfs_l)
    pool_sample = jnp.stack(ps_l)
    sgu_v_sample = jnp.stack(cs_l)
    return (yp, ys, k_prompt, v_prompt, logf_prompt, pool_prompt,
            k_sample, v_sample, logf_sample, pool_sample, sgu_v_sample)
```

```python
from contextlib import ExitStack
import numpy as np
import concourse.bass as bass
import concourse.mybir as mybir
from concourse.bass_types import AP
from concourse.bass_utils import run_bass_kernel_spmd

F32 = mybir.dt.float32
BF16 = mybir.dt.bfloat16
AF = mybir.ActivationFunctionType
ALU = mybir.AluOpType
AX = mybir.AxisListType

COMPUTE = ("pe", "dve", "act", "pool")
NDMASEM = 12
NEG = -30000.0
EPS = 1e-6
D = 4096
KC = 32
NH = 16
HD = 128
NCTX = 32
NT_MAIN = 1280
NT_H = 1296
IN0 = 12304


class T:
    __slots__ = ("t", "ws", "r", "name", "dram")

    def __init__(self, t, name="", dram=False):
        self.t = t
        self.ws = []
        self.r = []
        self.name = name
        self.dram = dram

    def __getitem__(self, k):
        return self.t[k]


class Op:
    __slots__ = ("eng", "fn", "deps", "sig", "tok", "dma")

    def __init__(self, eng, fn, dma):
        self.eng = eng
        self.fn = fn
        self.deps = []
        self.sig = False
        self.tok = None
        self.dma = dma


class Sched:
    def __init__(self, nc, es):
        self.nc = nc
        self.es = es
        self.ops = []
        self.eng = {"pe": nc.tensor, "dve": nc.vector, "act": nc.scalar, "pool": nc.gpsimd, "sp": nc.sync}
        self.n = 0

    def sb(self, shape, dt, name=None):
        self.n += 1
        name = name or f"sb{self.n}"
        return T(self.es.enter_context(self.nc.sbuf_tensor(name, list(shape), dt)), name)

    def ps(self, shape, dt=F32, name=None):
        self.n += 1
        name = name or f"ps{self.n}"
        return T(self.es.enter_context(self.nc.psum_tensor(name, list(shape), dt)), name)

    muted = False

    def op(self, eng, fn, reads=(), writes=(), dma=False, loose=False, dj=False):
        o = Op(eng, fn, dma)
        if self.muted:
            return o
        deps = []
        for t in reads:
            for (w, _) in t.ws:
                deps.append((w, "raw"))
        for t in writes:
            for (w, wdj) in t.ws:
                if not (dj and wdj):
                    deps.append((w, "waw"))
            for r in t.r:
                deps.append((r, "war"))
        seen = set()
        for d, kind in deps:
            if d is o or id(d) in seen:
                continue
            if (not d.dma) and (not dma) and d.eng == eng:
                if eng == "pe" or (kind != "raw" and loose):
                    continue
            seen.add(id(d))
            o.deps.append(d)
            d.sig = True
        for t in reads:
            if not dma:
                t.r = [r for r in t.r if r.dma or r.eng != eng]
            t.r.append(o)
        for t in writes:
            if dj:
                t.ws.append((o, True))
            else:
                t.ws = [(o, False)]
                t.r = []
        self.ops.append(o)
        return o

    def emit(self):
        nc, es = self.nc, self.es
        csem = {e: es.enter_context(nc.semaphore(f"c_{e}")) for e in COMPUTE}
        ccnt = {e: 0 for e in COMPUTE}
        qs = ("sp", "act", "pool")
        dsem = {q: [es.enter_context(nc.semaphore(f"d_{q}{i}")) for i in range(NDMASEM)] for q in qs}
        dcnt = {q: [0] * NDMASEM for q in qs}
        dn = {q: 0 for q in qs}
        known = {e: {} for e in ("pe", "dve", "act", "pool", "sp")}

        def wait(e, sem, val):
            k = known[e]
            key = id(sem)
            if k.get(key, 0) >= val:
                return
            k[key] = val
            self.eng[e].wait_ge(sem, val)

        for o in self.ops:
            e = o.eng
            for d in o.deps:
                sem, val = d.tok
                wait(e, sem, val)
            if o.dma:
                i = dn[e] % NDMASEM
                dn[e] += 1
                sem = dsem[e][i]
                if dcnt[e][i] > 0:
                    wait(e, sem, dcnt[e][i])
                inst = o.fn()
                dcnt[e][i] += 16
                inst.then_inc(sem, 16)
                o.tok = (sem, dcnt[e][i])
            else:
                inst = o.fn()
                if o.sig:
                    ccnt[e] += 1
                    inst.then_inc(csem[e], 1)
                    o.tok = (csem[e], ccnt[e])
        for q in qs:
            for i in range(NDMASEM):
                if dcnt[q][i] > 0:
                    wait("sp", dsem[q][i], dcnt[q][i])
        self.ops = []


class _Stop(Exception):
    pass


DEBUG = [False]
SKIP = set()
FLAGS = set()


def build_nc(limit=None):
    holder = {}
    try:
        _build_body(holder, limit)
    except _Stop:
        pass
    holder["s"].emit()
    holder["es"].close()
    return holder["nc"]


def _build_body(holder, limit):
    nc = bass.Bass("TRN2", target_bir_lowering=False)
    es = ExitStack()
    s = Sched(nc, es)
    holder.update(nc=nc, es=es, s=s)

    def phase_end(k):
        if limit is not None and k >= limit:
            raise _Stop()
        s.muted = (int(k) in SKIP)

    def din(name, shape, dt=F32):
        return T(nc.dram_tensor(name, list(shape), dt, kind="ExternalInput"), name, dram=True)

    def dout(name, shape, dt=F32):
        return T(nc.dram_tensor(name, list(shape), dt, kind="ExternalOutput"), name, dram=True)

    def dscr(name, shape, dt):
        return T(nc.dram_tensor(name, list(shape), dt, kind=("ExternalOutput" if DEBUG[0] else "Internal")), name, dram=True)

    def dbg_dump(name, t, ap, shape, dt=F32):
        if not DEBUG[0]:
            return
        d = T(nc.dram_tensor("dbg_" + name, list(shape), dt, kind="ExternalOutput"), name)
        s.op("sp", lambda: nc.sync.dma_start(out=d.t.ap(), in_=ap), reads=[t], writes=[d], dma=True)

    xall = din("xall", [35 * 128, D])
    csT = din("csT", [128, KC * 5])
    w_ada = din("w_ada", [2, D, 3 * D])
    b_adaT = din("b_adaT", [2, 128, 96])
    g_normT = din("g_normT", [2, 128, KC])
    w_in_ab = din("w_in_ab", [D, IN0])
    bf_bc = din("bf_bc", [128, NH])
    gq_bc = din("gq_bc", [128, HD])
    gk_bc = din("gk_bc", [128, HD])
    w_pool = din("w_pool", [4, 512, 512])
    ls_poolT = din("ls_poolT", [128, 16])
    w_out_ab = din("w_out_ab", [D, D])
    w_in_c = din("w_in_c", [D, 3 * D])
    gv_bc = din("gv_bc", [128, D])
    bv_bc = din("bv_bc", [128, D])
    wsT = din("wsT", [2, 128, NH * 128])
    bs_bc = din("bs_bc", [2, 128, NH * 128])
    w_out_c = din("w_out_c", [D, D])
    cache_k = din("cache_k", [4 * 4096, 2048])
    cache_v = din("cache_v", [4 * 4096, 2048])
    cache_lf = din("cache_lf", [4 * 4096, NH])
    spT = din("spT", [4, 16, 128, 16])
    meta_valid = din("meta_valid", [128, NCTX])
    meta_kneg = din("meta_kneg", [128, NCTX])
    meta_prev = din("meta_prev", [128, 1])
    corr = din("corr", [128, 4 * 16])
    consts = din("consts", [128, 4 * 128])
    selc = din("selc", [128, NH * 128])

    y_out = dout("y_out", [NT_MAIN, D])
    k_out = dout("k_out", [NT_MAIN, 2048])
    v_out = dout("v_out", [NT_MAIN, 2048])
    lf_out = dout("lf_out", [NT_MAIN, NH])
    pool_out = dout("pool_out", [5, 16, 128, 15])
    sguv_out = dout("sguv_out", [256, D])

    kT_s = dscr("kT_s", [NH, 128, 4352], BF16)
    v_s = dscr("v_s", [4352, 2048], BF16)
    qT_s = dscr("qT_s", [NH, 128, NT_MAIN], BF16)
    ubT_s = dscr("ubT_s", [16, 128, NT_H], F32)
    sgT_s = dscr("sgT_s", [KC, 128, NT_MAIN], BF16)
    y0_s = dscr("y0_s", [NT_MAIN, D], F32)
    vsc_s = dscr("vsc_s", [NT_MAIN, D], F32)
    ugT_s = dscr("ugT_s", [KC, 128, NT_MAIN], BF16)

    hT = s.sb([128, KC, NT_H], BF16, "hT")
    wbig = es.enter_context(nc.sbuf_tensor("wbig", [128, 4, KC, 128], BF16))
    wb = [T(wbig[:, i], f"wb{i}") for i in range(4)]
    hstat = [s.sb([128, 4], F32, f"hstat{i}") for i in range(2)]
    xt = s.sb([128, D], F32, "xt")
    xn = s.sb([128, D], BF16, "xn")
    sc8 = s.sb([128, 2048], F32, "sc8")
    ident = s.sb([128, 128], BF16, "ident")
    identf = s.sb([128, 128], F32, "identf")
    ones_bf = s.sb([128, 128], BF16, "ones_bf")
    cst = s.sb([128, 4 * 128], F32, "cst")
    sel = s.sb([128, NH * 128], BF16, "sel")
    scT = s.sb([128, KC * 5], BF16, "scT")
    csf = s.sb([128, KC * 5], F32, "csf")
    modT = [s.sb([128, 96, 5], F32, f"modT{l}") for l in range(2)]
    AT = [s.sb([128, KC, 5], F32, f"AT{l}") for l in range(2)]
    badaT = [s.sb([128, 96], F32, f"badaT{l}") for l in range(2)]
    gnT = [s.sb([128, KC], F32, f"gnT{l}") for l in range(2)]
    bf_sb = s.sb([128, NH], F32, "bf_sb")
    gq_sb = s.sb([128, HD], F32, "gq_sb")
    gk_sb = s.sb([128, HD], F32, "gk_sb")
    valid_sb = s.sb([128, NCTX], F32, "valid_sb")
    kneg_sb = s.sb([128, NCTX], F32, "kneg_sb")
    prev_sb = s.sb([128, 1], F32, "prev_sb")
    corr_sb = s.sb([128, 4, 16], F32, "corr_sb")
    lsT_sb = s.sb([128, 16], F32, "lsT_sb")
    kbias = s.sb([128, 34 + 32, NH], F32, "kbias")
    lf_all = s.sb([128, 34, NH], F32, "lf_all")
    lfm = s.sb([128, NH], F32, "lfm")
    F_all = s.sb([128, 34, NH], F32, "F_all")
    Faug = s.sb([128, NT_MAIN], BF16, "Faug")
    carry = s.sb([128, NH], F32, "carry")
    stat = s.sb([128, 16], F32, "stat")
    stg = [s.sb([128, 512], F32, f"stg{i}") for i in range(3)]
    stb = [s.sb([128, 1024], BF16, f"stb{i}") for i in range(3)]
    ckT = s.sb([128, 2048], BF16, "ckT")
    qTh = [s.sb([128, 1024], BF16, f"qTh{i}") for i in range(2)]
    mwide = s.sb([128, 896], F32, "mwide")
    maskadd_s = s.sb([128, 2, 64], F32, "maskadd_s")
    trimask = s.sb([128, 128], BF16, "trimask")
    trimf = s.sb([128, 128], F32, "trimf")
    wsm = s.sb([128, NH, 128], BF16, "wsm")
    bn = s.sb([128, 8, 6], F32, "bn")
    mv = s.sb([128, 4], F32, "mv")
    ybuf = s.sb([128, 10, 128], F32, "ybuf")

    pb = [s.ps([128, 512], F32, f"pb{i}") for i in range(8)]

    def pbf(i):
        return pb[i].t[:].bitcast(BF16)

    def dma(q, out_t, out_ap, in_t, in_ap):
        fn = (nc.gpsimd.dma_start if q == "pool" else nc.sync.dma_start)
        return s.op(q, lambda: fn(out=out_ap, in_=in_ap), reads=[in_t], writes=[out_t], dma=True, dj=out_t.dram)

    def V(fn, reads, writes, loose=False):
        return s.op("dve", fn, reads=reads, writes=writes, loose=loose)

    def A(fn, reads, writes, loose=False):
        return s.op("act", fn, reads=reads, writes=writes, loose=loose)

    def P(fn, reads, writes, loose=False):
        return s.op("pool", fn, reads=reads, writes=writes, loose=loose)

    def mm(ps_t, ps_ap, lhsT, rhs, start, stop, reads, skip=False):
        return s.op("pe", lambda: nc.tensor.matmul(ps_ap, lhsT=lhsT, rhs=rhs, start=start, stop=stop, skip_group_check=skip),
                    reads=reads, writes=[ps_t])

    def tr(ps_t, ps_ap, in_ap, id_ap, reads):
        return s.op("pe", lambda: nc.tensor.transpose(out=ps_ap, in_=in_ap, identity=id_ap), reads=reads, writes=[ps_t])

    dma("sp", cst, cst[:], consts, consts.t.ap())
    dma("pool", sel, sel[:], selc, selc.t.ap())
    P(lambda: nc.gpsimd.memset(identf[:], 1.0), [], [identf])
    P(lambda: nc.gpsimd.affine_select(out=identf[:], in_=identf[:], pattern=[[-1, 128]], compare_op=ALU.is_equal,
                                      fill=0.0, base=0, channel_multiplier=1), [identf], [identf])
    V(lambda: nc.vector.tensor_copy(out=ident[:], in_=identf[:]), [identf], [ident])
    V(lambda: nc.vector.memset(ones_bf[:], 1.0), [], [ones_bf])
    P(lambda: nc.gpsimd.memset(mwide[:], 0.0), [], [mwide])
    P(lambda: nc.gpsimd.affine_select(out=mwide[:], in_=mwide[:], pattern=[[1, 896]], compare_op=ALU.is_ge, fill=NEG,
                                      base=-384, channel_multiplier=-1), [mwide], [mwide])
    P(lambda: nc.gpsimd.memset(maskadd_s[:], NEG), [], [maskadd_s])
    P(lambda: nc.gpsimd.memset(maskadd_s[0:64, 0, :], 0.0), [maskadd_s], [maskadd_s])
    P(lambda: nc.gpsimd.memset(maskadd_s[64:128, 1, :], 0.0), [maskadd_s], [maskadd_s])
    P(lambda: nc.gpsimd.affine_select(out=maskadd_s[0:64, 0, :], in_=maskadd_s[0:64, 0, :], pattern=[[1, 64]],
                                      compare_op=ALU.is_ge, fill=NEG, base=0, channel_multiplier=-1), [maskadd_s], [maskadd_s])
    P(lambda: nc.gpsimd.affine_select(out=maskadd_s[64:128, 1, :], in_=maskadd_s[64:128, 1, :], pattern=[[1, 64]],
                                      compare_op=ALU.is_ge, fill=NEG, base=0, channel_multiplier=-1), [maskadd_s], [maskadd_s])
    P(lambda: nc.gpsimd.memset(trimf[:], 1.0), [], [trimf])
    P(lambda: nc.gpsimd.affine_select(out=trimf[:], in_=trimf[:], pattern=[[1, 128]], compare_op=ALU.is_ge,
                                      fill=0.0, base=0, channel_multiplier=-1), [trimf], [trimf])
    V(lambda: nc.vector.tensor_copy(out=trimask[:], in_=trimf[:]), [trimf], [trimask])
    for (sbt, dt_) in ((bf_sb, bf_bc), (gq_sb, gq_bc), (gk_sb, gk_bc), (valid_sb, meta_valid), (kneg_sb, meta_kneg),
                       (prev_sb, meta_prev), (lsT_sb, ls_poolT), (csf, csT)):
        dma("sp", sbt, sbt[:], dt_, dt_.t.ap())
    dma("sp", corr_sb, corr_sb[:], corr, corr.t.ap().rearrange("p (a b) -> p a b", b=16))
    for l in range(2):
        dma("sp", badaT[l], badaT[l][:], b_adaT, b_adaT.t.ap()[l])
        dma("sp", gnT[l], gnT[l][:], g_normT, g_normT.t.ap()[l])
    V(lambda: nc.vector.tensor_scalar(out=gq_sb[:], in0=gq_sb[:], scalar1=float(HD ** -0.5), scalar2=None, op0=ALU.mult),
      [gq_sb], [gq_sb])
    TRI_INC, TRI_REV, TRI_S, ONESF = (cst[:, 0:128], cst[:, 128:256], cst[:, 256:384], cst[:, 384:512])

    phase_end(0)
    A(lambda: nc.scalar.activation(out=scT[:], in_=csf[:], func=AF.Silu), [csf], [scT])
    wa = [(wb[i], wb[i].t[:].rearrange("p a b -> p (a b)")) for i in range(4)]
    cnt = 0
    for l in range(2):
        mps = pb[l]
        V(lambda mps=mps: nc.vector.memset(mps[:, :], 0.0), [], [mps])
        for kc in range(KC):
            for piece in range(3):
                Tb, view = wa[cnt % 4]
                cnt += 1
                dma("pool", Tb, view, w_ada, w_ada.t.ap()[l, kc * 128:(kc + 1) * 128, piece * 4096:(piece + 1) * 4096])
                for ch in range(32):
                    cg = piece * 32 + ch
                    mm(mps, mps[:, cg * 5:(cg + 1) * 5], view[:, ch * 128:(ch + 1) * 128], scT[:, kc * 5:(kc + 1) * 5],
                       False, kc == KC - 1, [Tb, scT], skip=True)
        mview = mps.t[:, 0:480].rearrange("p (c s) -> p c s", s=5)
        for sq in range(5):
            V(lambda l=l, sq=sq, mview=mview: nc.vector.tensor_tensor(out=modT[l][:, :, sq], in0=mview[:, :, sq], in1=badaT[l][:], op=ALU.add),
              [mps, badaT[l]], [modT[l]])
            V(lambda l=l, sq=sq: nc.vector.scalar_tensor_tensor(out=AT[l][:, :, sq], in0=modT[l][:, 32:64, sq], scalar=1.0, in1=gnT[l][:],
                                                                 op0=ALU.add, op1=ALU.mult), [modT[l], gnT[l]], [AT[l]])

    for l in range(2):
        dbg_dump(f"modT{l}", modT[l], modT[l][:], [128, 96, 5])
        dbg_dump(f"AT{l}", AT[l], AT[l][:], [128, KC, 5])
    phase_end(1)
    pbrr = [0]

    def next_pb(lo=0, hi=8):
        i = lo + pbrr[0] % (hi - lo)
        pbrr[0] += 1
        return i

    def norm_tile(src_t, src_ap, nrows, slices, l, col0):
        dma("sp", xt, xt[0:nrows, :], src_t, src_ap)
        A(lambda: nc.scalar.activation(out=xn[0:nrows, :], in_=xt[0:nrows, :], func=AF.Square, accum_out=stat[0:nrows, 0:1]),
          [xt], [xn, stat])
        A(lambda: nc.scalar.activation(out=stat[0:nrows, 1:2], in_=stat[0:nrows, 0:1], func=AF.Sqrt, bias=EPS, scale=1.0 / D),
          [stat], [stat])
        V(lambda: nc.vector.reciprocal(out=stat[0:nrows, 2:3], in_=stat[0:nrows, 1:2]), [stat], [stat])
        V(lambda: nc.vector.tensor_scalar(out=xn[0:nrows, :], in0=xt[0:nrows, :], scalar1=stat[0:nrows, 2:3], scalar2=None, op0=ALU.mult),
          [xt, stat], [xn])
        for c8 in range(4):
            bi = next_pb()
            bank = pbf(bi)
            for cc in range(8):
                ch = c8 * 8 + cc
                tr(pb[bi], bank[:, cc * 128:cc * 128 + nrows], xn[0:nrows, ch * 128:(ch + 1) * 128], ident[0:nrows, 0:nrows], [xn, ident])
            for cc in range(8):
                ch = c8 * 8 + cc
                for (lo, hi, sq) in slices:
                    V(lambda bank=bank, cc=cc, ch=ch, lo=lo, hi=hi, sq=sq: nc.vector.tensor_scalar(
                        out=hT[:, ch, col0 + lo:col0 + hi], in0=bank[:, cc * 128 + lo:cc * 128 + hi],
                        scalar1=AT[l][:, ch, sq:sq + 1], scalar2=modT[l][:, ch, sq:sq + 1], op0=ALU.mult, op1=ALU.add),
                      [pb[bi], AT[l], modT[l]], [hT], loose=True)

    wrr = [0]

    def wload(W, col0, ncols=128, rows=KC):
        Tb = wb[wrr[0] % 4]
        wrr[0] += 1
        dma("pool", Tb, Tb[:, 0:rows, 0:ncols], W, W.t.ap()[0:rows * 128, col0:col0 + ncols].rearrange("(kc p) n -> p kc n", p=128))
        return Tb

    def wload_pair(W, col0):
        if wrr[0] % 2:
            wrr[0] += 1
        i0 = wrr[0] % 4
        wload(W, col0)
        wload(W, col0 + 128)
        return i0

    def proj_tok_pair(i0, col, ntok, bi):
        out = pb[bi][0:ntok, 0:256].rearrange("p (a b) -> p a b", b=128)
        for kc in range(KC):
            mm(pb[bi], out, hT[:, kc, col:col + ntok], wbig[:, i0:i0 + 2, kc, :], kc == 0, kc == KC - 1, [hT, wb[i0], wb[i0 + 1]])

    def proj_tok(Tb, ncols, col, ntok, bi):
        for kc in range(KC):
            mm(pb[bi], pb[bi][0:ntok, 0:ncols], hT[:, kc, col:col + ntok], Tb[:, kc, 0:ncols], kc == 0, kc == KC - 1, [hT, Tb])

    def proj_feat(Tb, c0, col, n, bi):
        for kc in range(KC):
            mm(pb[bi], pb[bi][:, 0:n], Tb[:, kc, c0:c0 + 128], hT[:, kc, col:col + n], kc == 0, kc == KC - 1, [hT, Tb])

    def headnorm_pair(bi, g_sb, h0, scol, stages, kout_rows):
        raw, junk, kf, kb16 = stg[2], stg[0], stg[1], stb[0]
        A(lambda: nc.scalar.copy(out=raw[:, 0:256], in_=pb[bi][:, 0:256]), [pb[bi]], [raw])
        for i in range(2):
            st_ = hstat[i]
            sl = slice(i * 128, (i + 1) * 128)
            A(lambda st_=st_, sl=sl: nc.scalar.activation(out=junk[:, sl], in_=raw[:, sl], func=AF.Square, accum_out=st_[:, 0:1]), [raw], [junk, st_], loose=(i > 0))
            A(lambda st_=st_: nc.scalar.activation(out=st_[:, 1:2], in_=st_[:, 0:1], func=AF.Sqrt, bias=EPS, scale=1.0 / HD), [st_], [st_])
            V(lambda st_=st_: nc.vector.reciprocal(out=st_[:, 2:3], in_=st_[:, 1:2]), [st_], [st_])
            V(lambda st_=st_, sl=sl: nc.vector.scalar_tensor_tensor(out=kf[:, sl], in0=raw[:, sl], scalar=st_[:, 2:3], in1=g_sb[:], op0=ALU.mult, op1=ALU.mult),
              [raw, st_, g_sb], [kf], loose=(i > 0))
        A(lambda: nc.scalar.copy(out=kb16[:, 0:256], in_=kf[:, 0:256]), [kf], [kb16])
        if kout_rows is not None and "nokout" not in FLAGS:
            dma("sp", k_out, k_out.t.ap()[kout_rows:kout_rows + 128, h0 * 128:h0 * 128 + 256], kf, kf[:, 0:256])
        bj = next_pb()
        bank = pbf(bj)
        for i in range(2):
            tr(pb[bj], bank[:, i * 128:(i + 1) * 128], kb16[:, i * 128:(i + 1) * 128], ident[:], [kb16, ident])
        for i in range(2):
            A(lambda i=i: nc.scalar.copy(out=stages[i][:, scol:scol + 128], in_=bank[:, i * 128:(i + 1) * 128]), [pb[bj]], [stages[i]], loose=True)

    def kvf_for_tiles(tile_ids, cols_of, out_row_of):
        nt = len(tile_ids)
        if "noout" in FLAGS:
            out_row_of = {}
        for hp in range(NH // 2 if "noK" not in FLAGS else 0):
            h0 = 2 * hp
            i0 = wload_pair(w_in_ab, 2048 + h0 * 128)
            stages = (stb[1], stb[2])
            for i, ti in enumerate(tile_ids):
                bi = next_pb()
                proj_tok_pair(i0, cols_of[ti], 128, bi)
                headnorm_pair(bi, gk_sb, h0, (i % 8) * 128, stages, out_row_of.get(ti))
                if i % 8 == 7 or i == nt - 1:
                    n = (i % 8 + 1) * 128
                    t0 = tile_ids[i - (i % 8)]
                    for k in range(2):
                        dma("sp", kT_s, kT_s.t.ap()[h0 + k, :, t0 * 128:t0 * 128 + n], stages[k], stages[k][:, 0:n])
        for hp in range(NH // 2 if "noV" not in FLAGS else 0):
            h0 = 2 * hp
            i0 = wload_pair(w_in_ab, 4096 + h0 * 128)
            for i, ti in enumerate(tile_ids):
                bi = next_pb()
                proj_tok_pair(i0, cols_of[ti], 128, bi)
                vb = stb[0]
                has_out = out_row_of.get(ti) is not None and "novout" not in FLAGS
                if has_out:
                    vf = stg[2]
                    V(lambda bi=bi, vf=vf: nc.vector.tensor_copy(out=vf[:, 0:256], in_=pb[bi][:, 0:256]), [pb[bi]], [vf])
                    A(lambda vf=vf, vb=vb: nc.scalar.copy(out=vb[:, 0:256], in_=vf[:, 0:256]), [vf], [vb])
                    r0 = out_row_of[ti]
                    dma("sp", v_out, v_out.t.ap()[r0:r0 + 128, h0 * 128:h0 * 128 + 256], vf, vf[:, 0:256])
                else:
                    A(lambda bi=bi, vb=vb: nc.scalar.copy(out=vb[:, 0:256], in_=pb[bi][:, 0:256]), [pb[bi]], [vb])
                dma("sp", v_s, v_s.t.ap()[ti * 128:(ti + 1) * 128, h0 * 128:h0 * 128 + 256], vb, vb[:, 0:256])
        Tb = wload(w_in_ab, 6144, 16)
        for i, ti in enumerate(tile_ids if "noF" not in FLAGS else []):
            bi = next_pb()
            proj_tok(Tb, 16, cols_of[ti], 128, bi)
            t1 = stg[2]
            V(lambda bi=bi, t1=t1: nc.vector.tensor_tensor(out=t1[:, 0:16], in0=pb[bi][:, 0:16], in1=bf_sb[:], op=ALU.add), [pb[bi], bf_sb], [t1])
            A(lambda t1=t1: nc.scalar.activation(out=t1[:, 16:32], in_=t1[:, 0:16], func=AF.Exp, scale=-1.0), [t1], [t1])
            A(lambda t1=t1: nc.scalar.activation(out=t1[:, 32:48], in_=t1[:, 16:32], func=AF.Ln, bias=1.0, scale=1.0), [t1], [t1])
            V(lambda t1=t1, ti=ti: nc.vector.tensor_scalar(out=lf_all[:, ti, :], in0=t1[:, 32:48], scalar1=-1.0, scalar2=None, op0=ALU.mult),
              [t1], [lf_all], loose=True)
            if out_row_of.get(ti) is not None and "nolfout" not in FLAGS:
                r0 = out_row_of[ti]
                dma("sp", lf_out, lf_out.t.ap()[r0:r0 + 128, :], lf_all, lf_all[:, ti, :])

    for g in range(3):
        tiles = list(range(8 * g, 8 * g + 8))
        for i, ti in enumerate(tiles):
            norm_tile(xall, xall.t.ap()[ti * 128:(ti + 1) * 128, :], 128, [(0, 128, 0)], 0, i * 128)
        kvf_for_tiles(tiles, {ti: i * 128 for i, ti in enumerate(tiles)}, {})

    phase_end(2)
    main_tiles = list(range(24, 34))
    cols_of = {ti: (ti - 24) * 128 for ti in main_tiles}
    out_row_of = {ti: (ti - 24) * 128 for ti in main_tiles}

    def slices_of(ti):
        if ti < 32 or ti == 34:
            return [(0, 128 if ti < 32 else 16, 0)]
        base = 1 + (ti - 32) * 2
        return [(0, 64, base), (64, 128, base + 1)]

    for ti in main_tiles:
        norm_tile(xall, xall.t.ap()[ti * 128:(ti + 1) * 128, :], 128, slices_of(ti), 0, cols_of[ti])
    norm_tile(xall, xall.t.ap()[34 * 128:34 * 128 + 16, :], 16, [(0, 16, 0)], 0, 1280)

    phase_end(2.1)
    for hp in range(NH // 2):
        h0 = 2 * hp
        i0 = wload_pair(w_in_ab, h0 * 128)
        stages = (stb[1], stb[2])
        for half in range(2):
            tl = main_tiles[half * 8:(half + 1) * 8]
            for i, ti in enumerate(tl):
                bi = next_pb()
                proj_tok_pair(i0, cols_of[ti], 128, bi)
                headnorm_pair(bi, gq_sb, h0, i * 128, stages, None)
            n = len(tl) * 128
            for k in range(2):
                dma("sp", qT_s, qT_s.t.ap()[h0 + k, :, half * 1024:half * 1024 + n], stages[k], stages[k][:, 0:n])
    phase_end(2.2)
    kvf_for_tiles(main_tiles, cols_of, out_row_of)
    phase_end(2.3)
    rngs_h = [(0, 512), (512, 512), (1024, 272)]
    rngs_m = [(0, 512), (512, 512), (1024, 256)]
    for c in range(16):
        Tb = wload(w_in_ab, 6160 + c * 128)
        for (c0, n) in rngs_h:
            bi = next_pb()
            proj_feat(Tb, 0, c0, n, bi)
            st_ = stg[c % 2]
            A(lambda bi=bi, st_=st_, n=n: nc.scalar.copy(out=st_[:, 0:n], in_=pb[bi][:, 0:n]), [pb[bi]], [st_])
            dma("sp", ubT_s, ubT_s.t.ap()[c, :, c0:c0 + n], st_, st_[:, 0:n])
    phase_end(2.4)
    for c in range(KC):
        Tb = wload(w_in_ab, 8208 + c * 128)
        for (c0, n) in rngs_m:
            bi = next_pb()
            proj_feat(Tb, 0, c0, n, bi)
            sb_ = stb[c % 3]
            A(lambda bi=bi, sb_=sb_, n=n: nc.scalar.activation(out=sb_[:, 0:n], in_=pb[bi][:, 0:n], func=AF.Silu), [pb[bi]], [sb_])
            dma("sp", sgT_s, sgT_s.t.ap()[c, :, c0:c0 + n], sb_, sb_[:, 0:n])

    phase_end(3)
    V(lambda: nc.vector.memset(carry[:], 0.0), [], [carry])
    for ti in range(NCTX):
        V(lambda ti=ti: nc.vector.tensor_scalar(out=lfm[:], in0=lf_all[:, ti, :], scalar1=valid_sb[:, ti:ti + 1], scalar2=None, op0=ALU.mult),
          [lf_all, valid_sb], [lfm])
        b0, b1 = next_pb(), next_pb()
        mm(pb[b0], pb[b0][:, 0:16], TRI_INC, lfm[:], True, True, [cst, lfm])
        mm(pb[b1], pb[b1][:, 0:16], ONESF, lfm[:], True, True, [cst, lfm])
        V(lambda ti=ti, b0=b0: nc.vector.tensor_tensor(out=F_all[:, ti, :], in0=pb[b0][:, 0:16], in1=carry[:], op=ALU.add), [pb[b0], carry], [F_all], loose=True)
        V(lambda b1=b1: nc.vector.tensor_tensor(out=carry[:], in0=pb[b1][:, 0:16], in1=carry[:], op=ALU.add), [pb[b1], carry], [carry])
        V(lambda ti=ti: nc.vector.tensor_scalar(out=kbias[:, ti, :], in0=F_all[:, ti, :], scalar1=-1.0, scalar2=kneg_sb[:, ti:ti + 1],
                                                op0=ALU.mult, op1=ALU.add), [F_all, kneg_sb], [kbias], loose=True)
    for ti in (32, 33):
        b0 = next_pb()
        mm(pb[b0], pb[b0][:, 0:16], TRI_S, lf_all[:, ti, :], True, True, [cst, lf_all])
        V(lambda ti=ti, b0=b0: nc.vector.tensor_copy(out=F_all[:, ti, :], in_=pb[b0][:, 0:16]), [pb[b0]], [F_all], loose=True)
        V(lambda ti=ti: nc.vector.tensor_scalar(out=kbias[:, ti, :], in0=F_all[:, ti, :], scalar1=-1.0, scalar2=None, op0=ALU.mult), [F_all], [kbias], loose=True)
    for i, ti in enumerate(main_tiles):
        fs = stb[0]
        r = stg[0]
        V(lambda fs=fs: nc.vector.memset(fs[:, 0:128], 0.0), [], [fs])
        V(lambda fs=fs, ti=ti: nc.vector.tensor_copy(out=fs[:, 0:16], in_=F_all[:, ti, :]), [F_all], [fs])
        V(lambda fs=fs, ti=ti, r=r: nc.vector.tensor_tensor(out=r[:, 0:16], in0=F_all[:, ti, :], in1=fs[:, 0:16], op=ALU.subtract), [F_all, fs], [r])
        V(lambda fs=fs, r=r: nc.vector.tensor_copy(out=fs[:, 32:48], in_=r[:, 0:16]), [r], [fs])
        V(lambda fs=fs, r=r: nc.vector.tensor_tensor(out=r[:, 16:32], in0=r[:, 0:16], in1=fs[:, 32:48], op=ALU.subtract), [r, fs], [r])
        V(lambda fs=fs, r=r: nc.vector.tensor_copy(out=fs[:, 64:80], in_=r[:, 16:32]), [r], [fs])
        bj = next_pb()
        tr(pb[bj], pbf(bj)[:, 0:128], fs[:, 0:128], ident[:], [fs, ident])
        A(lambda bj=bj, i=i: nc.scalar.copy(out=Faug[:, i * 128:(i + 1) * 128], in_=pbf(bj)[:, 0:128]), [pb[bj]], [Faug], loose=True)

    dbg_dump("lf_all", lf_all, lf_all[:], [128, 34, NH])
    dbg_dump("F_all", F_all, F_all[:], [128, 34, NH])
    dbg_dump("kbias", kbias, kbias[:, 0:34, :], [128, 34, NH])
    dbg_dump("Faug", Faug, Faug[:], [128, NT_MAIN], BF16)
    phase_end(4)
    mixedT = hT

    for h in range(NH):
        kTh = wb[(2 * h) % 4]
        vh = wb[(2 * h + 1) % 4]
        kview = kTh.t[:].rearrange("p a b -> p (a b)")
        dma("sp", kTh, kview, kT_s, kT_s.t.ap()[h, :, 0:4096])
        dma("sp", vh, vh[:], v_s, v_s.t.ap()[0:4096, h * 128:(h + 1) * 128].rearrange("(t p) d -> p t d", p=128))
        qh = qTh[h % 2]
        dma("sp", qh, qh[:], qT_s, qT_s.t.ap()[h, :, 0:1024])
        for qt in range(2):
            OT, SM = pb[4 + 2 * (qt % 2)], pb[5 + 2 * (qt % 2)]
            kbs = [kb for kb in range(32) if not (kb >= 24 and (kb - 24) > 4 * qt + 3)]
            for n_, kb in enumerate(kbs):
                bi = n_ % 4
                S = pb[bi]
                mm(S, S[:, :], kview[:, kb * 128:(kb + 1) * 128], qh[:, qt * 512:(qt + 1) * 512], True, False, [kTh, qh])
                mm(S, S[:, :], sel[:, h * 128:(h + 1) * 128], Faug[:, qt * 512:(qt + 1) * 512], False, True, [sel, Faug])
                Pt = stb[n_ % 3]
                d = kb - 24 - 4 * qt
                if kb >= 24 and 0 <= d <= 3:
                    sm_ = stg[n_ % 3]
                    V(lambda S=S, sm_=sm_, d=d: nc.vector.tensor_tensor(out=sm_[:], in0=S[:, :], in1=mwide[:, 384 - 128 * d:896 - 128 * d], op=ALU.add),
                      [S, mwide], [sm_])
                    A(lambda sm_=sm_, Pt=Pt, kb=kb, h=h: nc.scalar.activation(out=Pt[:, 0:512], in_=sm_[:], func=AF.Exp, bias=kbias[:, kb, h:h + 1], scale=1.0),
                      [sm_, kbias], [Pt])
                else:
                    A(lambda S=S, Pt=Pt, kb=kb, h=h: nc.scalar.activation(out=Pt[:, 0:512], in_=S[:, :], func=AF.Exp, bias=kbias[:, kb, h:h + 1], scale=1.0),
                      [S, kbias], [Pt])
                first, last = (n_ == 0), (n_ == len(kbs) - 1)
                mm(OT, OT[:, :], vh[:, kb, :], Pt[:, 0:512], first, last, [vh, Pt])
                mm(SM, SM[:, :], ones_bf[:], Pt[:, 0:512], first, last, [ones_bf, Pt])
            rs = stg[qt % 2]
            V(lambda SM=SM, rs=rs: nc.vector.reciprocal(out=rs[:], in_=SM[:, :]), [SM], [rs])
            V(lambda OT=OT, rs=rs: nc.vector.tensor_tensor(out=rs[:], in0=OT[:, :], in1=rs[:], op=ALU.mult), [OT, rs], [rs])
            sgt = stb[qt % 2]
            dma("sp", sgt, sgt[:, 512:1024], sgT_s, sgT_s.t.ap()[h, :, qt * 512:(qt + 1) * 512])
            V(lambda rs=rs, sgt=sgt, h=h, qt=qt: nc.vector.tensor_tensor(out=mixedT[:, h, qt * 512:(qt + 1) * 512], in0=rs[:], in1=sgt[:, 512:1024], op=ALU.mult),
              [rs, sgt], [mixedT], loose=True)

    phase_end(5)
    for sq in range(4):
        ti = 32 + sq // 2
        a_ = sq % 2
        tcol = 1024 + sq * 64
        clf = sc8
        dma("sp", clf, clf[:, 0:512].rearrange("p (b h) -> p b h", h=NH), cache_lf,
            cache_lf.t.ap()[sq * 4096:(sq + 1) * 4096, :].rearrange("(b p) h -> p b h", p=128))
        V(lambda: nc.vector.memset(carry[:], 0.0), [], [carry])
        for blk in range(31, -1, -1):
            b0, b1 = next_pb(0, 2), next_pb(0, 2)
            mm(pb[b0], pb[b0][:, 0:16], TRI_REV, clf[:, blk * 16:(blk + 1) * 16], True, True, [cst, clf])
            V(lambda blk=blk, b0=b0: nc.vector.tensor_tensor(out=kbias[:, 34 + blk, :], in0=pb[b0][:, 0:16], in1=carry[:], op=ALU.add),
              [pb[b0], carry], [kbias], loose=True)
            mm(pb[b1], pb[b1][:, 0:16], ONESF, clf[:, blk * 16:(blk + 1) * 16], True, True, [cst, clf])
            V(lambda b1=b1: nc.vector.tensor_tensor(out=carry[:], in0=pb[b1][:, 0:16], in1=carry[:], op=ALU.add), [pb[b1], carry], [carry])
        for ib in range(4, 8):
            V(lambda ib=ib: nc.vector.memset(pb[ib][:, :], 0.0), [], [pb[ib]])
        qs_ = qTh[sq % 2]
        qsv = qs_.t[:].rearrange("p (h t) -> p h t", t=64)
        dma("sp", qs_, qsv, qT_s, qT_s.t.ap()[:, :, tcol:tcol + 64].rearrange("h d t -> d h t"))
        def bufs(blk):
            ck = wb[(2 * blk) % 4]
            cv = wb[(2 * blk + 1) % 4]
            return ck, cv, ck.t[:].rearrange("p a b -> p (a b)"), cv.t[:].rearrange("p a b -> p (a b)")

        def load_blk(blk):
            ck, cv, ckv, cvv = bufs(blk)
            if blk < 32:
                r0 = sq * 4096 + blk * 128
                dma("pool", ck, ckv[:, 0:2048], cache_k, cache_k.t.ap()[r0:r0 + 128, :])
                dma("pool", cv, cvv[:, 0:2048], cache_v, cache_v.t.ap()[r0:r0 + 128, :])
            else:
                dma("sp", cv, cvv[:, 0:2048], v_s, v_s.t.ap()[ti * 128:(ti + 1) * 128, :])

        def stage_kT(blk):
            ck, cv, ckv, cvv = bufs(blk)
            if blk < 32:
                for hh in range(NH):
                    bank = pbf(hh // 8)
                    tr(pb[hh // 8], bank[:, (hh % 8) * 128:(hh % 8 + 1) * 128], ckv[:, hh * 128:(hh + 1) * 128], ident[:], [ck, ident])
                A(lambda: nc.scalar.copy(out=ckT[:, 0:1024], in_=pbf(0)[:, :]), [pb[0]], [ckT])
                V(lambda: nc.vector.tensor_copy(out=ckT[:, 1024:2048], in_=pbf(1)[:, :]), [pb[1]], [ckT], loose=True)
            else:
                dma("sp", ckT, ckT[:].rearrange("p (h t) -> p h t", t=128), kT_s,
                    kT_s.t.ap()[:, :, ti * 128:(ti + 1) * 128].rearrange("h d t -> d h t"))

        def qk(blk):
            for hh in range(NH):
                S = pb[2 + hh // 8]
                mm(S, S[:, (hh % 8) * 64:(hh % 8 + 1) * 64], ckT[:, hh * 128:(hh + 1) * 128], qsv[:, hh, :], True, True, [ckT, qs_])

        def exps(blk):
            own = (blk == 32)
            kb_idx = ti if own else 34 + blk
            Pst = stb[blk % 3]
            for hh in range(NH):
                S = pb[2 + hh // 8]
                Sv = S[:, (hh % 8) * 64:(hh % 8 + 1) * 64]
                Pv = Pst[:, hh * 64:(hh + 1) * 64]
                if own:
                    sm_ = stg[2]
                    V(lambda Sv=Sv, sm_=sm_, hh=hh, a_=a_: nc.vector.tensor_tensor(out=sm_[:, (hh % 8) * 64:(hh % 8 + 1) * 64], in0=Sv, in1=maskadd_s[:, a_, :], op=ALU.add),
                      [S, maskadd_s], [sm_], loose=True)
                    A(lambda sm_=sm_, Pv=Pv, hh=hh, kb_idx=kb_idx: nc.scalar.activation(out=Pv, in_=sm_[:, (hh % 8) * 64:(hh % 8 + 1) * 64], func=AF.Exp,
                                                                                          bias=kbias[:, kb_idx, hh:hh + 1], scale=1.0), [sm_, kbias], [Pst], loose=True)
                else:
                    A(lambda Sv=Sv, Pv=Pv, hh=hh, kb_idx=kb_idx: nc.scalar.activation(out=Pv, in_=Sv, func=AF.Exp, bias=kbias[:, kb_idx, hh:hh + 1], scale=1.0),
                      [S, kbias], [Pst], loose=True)

        def pv(blk):
            own = (blk == 32)
            ck, cv, ckv, cvv = bufs(blk)
            Pst = stb[blk % 3]
            for hh in range(NH):
                OT, SM = pb[4 + hh // 8], pb[6 + hh // 8]
                reg = slice((hh % 8) * 64, (hh % 8 + 1) * 64)
                Pv = Pst[:, hh * 64:(hh + 1) * 64]
                mm(OT, OT[:, reg], cvv[:, hh * 128:(hh + 1) * 128], Pv, False, own, [cv, Pst], skip=True)
                mm(SM, SM[:, reg], ones_bf[:], Pv, False, own, [ones_bf, Pst], skip=True)

        load_blk(0)
        stage_kT(0)
        load_blk(1)
        for blk in range(33):
            qk(blk)
            exps(blk)
            if blk + 1 <= 32:
                stage_kT(blk + 1)
            pv(blk)
            if blk + 2 <= 32:
                load_blk(blk + 2)
        sgs = stb[0]
        dma("sp", sgs, sgs[:].rearrange("p (c t) -> p c t", t=64), sgT_s, sgT_s.t.ap()[0:16, :, tcol:tcol + 64].rearrange("c p t -> p c t"))
        for g8 in range(2):
            rs = stg[g8]
            V(lambda rs=rs, g8=g8: nc.vector.reciprocal(out=rs[:], in_=pb[6 + g8][:, :]), [pb[6 + g8]], [rs])
            V(lambda rs=rs, g8=g8: nc.vector.tensor_tensor(out=rs[:], in0=pb[4 + g8][:, :], in1=rs[:], op=ALU.mult), [pb[4 + g8], rs], [rs])
            V(lambda rs=rs, g8=g8, sgs=sgs, tcol=tcol: nc.vector.tensor_tensor(out=mixedT[:, g8 * 8:(g8 + 1) * 8, tcol:tcol + 64],
                                                                    in0=rs[:].rearrange("p (c t) -> p c t", t=64),
                                                                    in1=sgs[:, g8 * 512:(g8 + 1) * 512].rearrange("p (c t) -> p c t", t=64), op=ALU.mult),
              [rs, sgs], [mixedT], loose=True)

    phase_end(6)
    EP, SA, SB_, ES, SSA, SSB = 0, 1040, 2080, 3120, 3440, 3760
    dTp = xn.t[:, 0:4096].rearrange("p (k t) -> p k t", t=1024)
    dTs = stb[2][:, 0:1024].rearrange("p (k t) -> p k t", t=256)
    for gi in range(4):
        w = 2 ** (gi + 1)
        for k4 in range(4):
            c = gi * 4 + k4
            dma("sp", xt, xt[:, EP:EP + 16], ubT_s, ubT_s.t.ap()[c, :, 1280:1296])
            dma("sp", xt, xt[:, EP + 16:EP + 1040], ubT_s, ubT_s.t.ap()[c, :, 0:1024])
            esv = xt.t[:, ES:ES + 320].rearrange("p (s t) -> p s t", t=80)
            dma("sp", xt, esv[:, :, 0:16], spT, spT.t.ap()[:, c, :, :].rearrange("s p t -> p s t"))
            dma("sp", xt, esv[:, :, 16:80], ubT_s, ubT_s.t.ap()[c, :, 1024:1280].rearrange("p (s t) -> p s t", t=64))
            V(lambda: nc.vector.tensor_scalar(out=xt[:, EP:EP + 16], in0=xt[:, EP:EP + 16], scalar1=prev_sb[:, 0:1], scalar2=None, op0=ALU.mult),
              [xt, prev_sb], [xt])
            dma("sp", pool_out, pool_out.t.ap()[0, c, :, :], xt, xt[:, EP + 1025:EP + 1040])
            dma("sp", pool_out, pool_out.t.ap()[1:5, c, :, :].rearrange("s p t -> p s t"), xt, esv[:, :, 65:80])
            src_p, src_s = EP, ES
            dst_seq = [(SA, SSA), (SB_, SSB), (SA, SSA), (SB_, SSB)]
            sh = 1
            for lvl in range(gi + 1):
                dp, ds = dst_seq[lvl]
                lo = 2 * sh - 1
                V(lambda dp=dp, src_p=src_p, lo=lo, sh=sh: nc.vector.tensor_tensor(out=xt[:, dp + lo:dp + 1040], in0=xt[:, src_p + lo:src_p + 1040],
                                                                                   in1=xt[:, src_p + lo - sh:src_p + 1040 - sh], op=ALU.add), [xt], [xt])
                sv_ = xt.t[:, src_s:src_s + 320].rearrange("p (s t) -> p s t", t=80)
                dv_ = xt.t[:, ds:ds + 320].rearrange("p (s t) -> p s t", t=80)
                V(lambda sv_=sv_, dv_=dv_, lo=lo, sh=sh: nc.vector.tensor_tensor(out=dv_[:, :, lo:80], in0=sv_[:, :, lo:80], in1=sv_[:, :, lo - sh:80 - sh], op=ALU.add),
                  [xt], [xt])
                src_p, src_s = dp, ds
                sh *= 2
            V(lambda src_p=src_p, gi=gi: nc.vector.tensor_tensor(out=xt[:, src_p + 16:src_p + 32], in0=xt[:, src_p + 16:src_p + 32], in1=corr_sb[:, gi, :], op=ALU.mult),
              [xt, corr_sb], [xt])
            V(lambda src_p=src_p, k4=k4, w=w: nc.vector.scalar_tensor_tensor(out=dTp[:, k4, :], in0=xt[:, src_p + 16:src_p + 1040], scalar=1.0 / w,
                                                                            in1=xt[:, EP + 16:EP + 1040], op0=ALU.mult, op1=ALU.subtract), [xt], [xn], loose=True)
            sv_ = xt.t[:, src_s:src_s + 320].rearrange("p (s t) -> p s t", t=80)
            V(lambda sv_=sv_, k4=k4, w=w, esv=esv: nc.vector.scalar_tensor_tensor(out=dTs[:, k4, :].rearrange("p (s t) -> p s t", t=64), in0=sv_[:, :, 16:80], scalar=1.0 / w,
                                                                                 in1=esv[:, :, 16:80], op0=ALU.mult, op1=ALU.subtract), [xt], [stb[2]], loose=True)
        Tb = wb[wrr[0] % 4]
        wrr[0] += 1
        dma("pool", Tb, Tb.t[:].rearrange("p a b -> p (a b)")[:, 0:2048].rearrange("p (k e) -> p k e", e=512), w_pool,
            w_pool.t.ap()[gi].rearrange("(k p) e -> p k e", p=128))
        wpv = Tb.t[:].rearrange("p a b -> p (a b)")[:, 0:2048].rearrange("p (k e) -> p k e", e=512)
        for ec in range(4):
            c = gi * 4 + ec
            for (c0, n, src) in ((0, 512, "p"), (512, 512, "p"), (1024, 256, "s")):
                bi = next_pb()
                for k in range(4):
                    rhs = dTp[:, k, c0:c0 + n] if src == "p" else dTs[:, k, :]
                    mm(pb[bi], pb[bi][:, 0:n], wpv[:, k, ec * 128:(ec + 1) * 128], rhs, k == 0, k == 3, [Tb, xn, stb[2]])
                sgt = stb[(ec) % 2]
                dma("sp", sgt, sgt[:, 0:n], sgT_s, sgT_s.t.ap()[16 + c, :, c0:c0 + n])
                V(lambda bi=bi, n=n, c=c, c0=c0, sgt=sgt: nc.vector.scalar_tensor_tensor(out=mixedT[:, 16 + c, c0:c0 + n], in0=pb[bi][:, 0:n], scalar=lsT_sb[:, c:c + 1],
                                                                                        in1=sgt[:, 0:n], op0=ALU.mult, op1=ALU.mult), [pb[bi], lsT_sb, sgt], [mixedT], loose=True)

    phase_end(7)
    seq_slices_m = [(0, 512, [(0, 512, 0)]), (512, 512, [(0, 512, 0)]), (1024, 256, [(0, 64, 1), (64, 128, 2), (128, 192, 3), (192, 256, 4)])]

    def out_proj(W, l, xsrc_t, xsrc_ap_rows, dst_t):
        for cb in range(KC):
            Tb = wload(W, cb * 128)
            dma("sp", ybuf, ybuf[:], xsrc_t, xsrc_ap_rows[:, cb * 128:(cb + 1) * 128].rearrange("(t p) c -> p t c", p=128))
            for (c0, n, sl) in seq_slices_m:
                bi = next_pb(0, 4)
                for kc in range(KC):
                    mm(pb[bi], pb[bi][:, 0:n], Tb[:, kc, :], mixedT[:, kc, c0:c0 + n], kc == 0, kc == KC - 1, [Tb, mixedT])
                og = stg[bi % 3]
                for (lo, hi, sq) in sl:
                    V(lambda bi=bi, og=og, lo=lo, hi=hi, sq=sq, cb=cb: nc.vector.tensor_scalar(out=og[:, lo:hi], in0=pb[bi][:, lo:hi],
                                                                                              scalar1=modT[l][:, 64 + cb, sq:sq + 1], scalar2=None, op0=ALU.mult),
                      [pb[bi], modT[l]], [og], loose=True)
                for t in range(n // 128):
                    bj = next_pb(4, 8)
                    tile_i = c0 // 128 + t
                    tr(pb[bj], pb[bj][:, 0:128], og[:, t * 128:(t + 1) * 128], identf[:], [og, identf])
                    V(lambda bj=bj, tile_i=tile_i: nc.vector.tensor_tensor(out=ybuf[:, tile_i, :], in0=pb[bj][:, 0:128], in1=ybuf[:, tile_i, :], op=ALU.add),
                      [pb[bj], ybuf], [ybuf])
            dma("sp", dst_t, dst_t.t.ap()[:, cb * 128:(cb + 1) * 128].rearrange("(t p) c -> p t c", p=128), ybuf, ybuf[:])

    out_proj(w_out_ab, 0, xall, xall.t.ap()[3072:4352, :], y0_s)

    phase_end(8)
    for i in range(10):
        norm_tile(y0_s, y0_s.t.ap()[i * 128:(i + 1) * 128, :], 128, slices_of(24 + i), 1, i * 128)
    for c in range(KC):
        Tu = wload(w_in_c, c * 128)
        Tg = wload(w_in_c, 8192 + c * 128)
        for (c0, n) in rngs_m:
            bu, bg = next_pb(), next_pb()
            proj_feat(Tu, 0, c0, n, bu)
            proj_feat(Tg, 0, c0, n, bg)
            su, sg_ = stg[0], stg[1]
            A(lambda bu=bu, su=su, n=n: nc.scalar.activation(out=su[:, 0:n], in_=pb[bu][:, 0:n], func=AF.Gelu), [pb[bu]], [su])
            A(lambda bg=bg, sg_=sg_, n=n: nc.scalar.activation(out=sg_[:, 0:n], in_=pb[bg][:, 0:n], func=AF.Silu), [pb[bg]], [sg_])
            ub_ = stb[c % 3]
            V(lambda su=su, sg_=sg_, ub_=ub_, n=n: nc.vector.tensor_tensor(out=ub_[:, 0:n], in0=su[:, 0:n], in1=sg_[:, 0:n], op=ALU.mult), [su, sg_], [ub_])
            dma("sp", ugT_s, ugT_s.t.ap()[c, :, c0:c0 + n], ub_, ub_[:, 0:n])
    for cp in range(KC // 2):
        i0 = wload_pair(w_in_c, 4096 + cp * 256)
        for i in range(10):
            bi = next_pb()
            proj_tok_pair(i0, i * 128, 128, bi)
            sv2 = stg[i % 3]
            A(lambda bi=bi, sv2=sv2: nc.scalar.activation(out=sv2[:, 0:256], in_=pb[bi][:, 0:256], func=AF.Gelu), [pb[bi]], [sv2])
            dma("sp", vsc_s, vsc_s.t.ap()[i * 128:(i + 1) * 128, cp * 256:(cp + 1) * 256], sv2, sv2[:, 0:256])
    mixed2T = hT
    for i in range(10):
        samp = i >= 8
        if i == 0 or i == 8:
            v_ = 1 if samp else 0
            dma("pool", wsm, wsm[:].rearrange("p g t -> p (g t)"), wsT, wsT.t.ap()[v_])
            for g in range(NH):
                V(lambda g=g: nc.vector.tensor_tensor(out=wsm[:, g, :], in0=wsm[:, g, :], in1=trimask[:], op=ALU.mult), [wsm, trimask], [wsm])
            dma("sp", sc8, sc8[:], bs_bc, bs_bc.t.ap()[v_])
        dma("sp", xt, xt[:], vsc_s, vsc_s.t.ap()[i * 128:(i + 1) * 128, :])
        for b8 in range(8):
            V(lambda b8=b8: nc.vector.bn_stats(out=bn[:, b8, :], in_=xt[:, b8 * 512:(b8 + 1) * 512]), [xt], [bn])
        V(lambda: nc.vector.bn_aggr(out=mv[:, 0:2], in_=bn[:].rearrange("p a b -> p (a b)")), [bn], [mv])
        A(lambda: nc.scalar.activation(out=mv[:, 2:3], in_=mv[:, 1:2], func=AF.Sqrt, bias=EPS, scale=1.0), [mv], [mv])
        V(lambda: nc.vector.reciprocal(out=mv[:, 3:4], in_=mv[:, 2:3]), [mv], [mv])
        V(lambda: nc.vector.tensor_scalar(out=xt[:], in0=xt[:], scalar1=mv[:, 0:1], scalar2=mv[:, 3:4], op0=ALU.subtract, op1=ALU.mult), [xt, mv], [xt])
        for b8 in range(8):
            gb, bb = stg[0], stg[1]
            dma("sp", gb, gb[:], gv_bc, gv_bc.t.ap()[:, b8 * 512:(b8 + 1) * 512])
            dma("sp", bb, bb[:], bv_bc, bv_bc.t.ap()[:, b8 * 512:(b8 + 1) * 512])
            V(lambda b8=b8, gb=gb: nc.vector.tensor_tensor(out=xt[:, b8 * 512:(b8 + 1) * 512], in0=xt[:, b8 * 512:(b8 + 1) * 512], in1=gb[:], op=ALU.mult), [xt, gb], [xt])
            V(lambda b8=b8, bb=bb: nc.vector.tensor_tensor(out=xt[:, b8 * 512:(b8 + 1) * 512], in0=xt[:, b8 * 512:(b8 + 1) * 512], in1=bb[:], op=ALU.add), [xt, bb], [xt])
        A(lambda: nc.scalar.copy(out=xn[:], in_=xt[:]), [xt], [xn])
        if samp:
            dma("sp", sguv_out, sguv_out.t.ap()[(i - 8) * 128:(i - 7) * 128, :], xt, xt[:])
        ugt = wb[wrr[0] % 4]
        wrr[0] += 1
        dma("sp", ugt, ugt[:], ugT_s, ugT_s.t.ap()[:, :, i * 128:(i + 1) * 128].rearrange("c p t -> p c t"))
        for c4 in range(8):
            bi = next_pb()
            for cc in range(4):
                c = c4 * 4 + cc
                mm(pb[bi], pb[bi][:, cc * 128:(cc + 1) * 128], xn[:, c * 128:(c + 1) * 128], wsm[:, c // 2, :], True, True, [xn, wsm])
            tmp = stg[2]
            V(lambda bi=bi, tmp=tmp, c4=c4: nc.vector.tensor_tensor(out=tmp[:].rearrange("p (c t) -> p c t", t=128), in0=pb[bi][:, :].rearrange("p (c t) -> p c t", t=128),
                                                                  in1=sc8[:, c4 * 256:(c4 + 1) * 256].rearrange("p (g o t) -> p g o t", o=1, t=128).to_broadcast([128, 2, 2, 128]).rearrange("p g o t -> p (g o) t")
                                                                  if False else sc8[:, 0:512].rearrange("p (c t) -> p c t", t=128), op=ALU.add), [pb[bi], sc8], [tmp]) if False else None
            for cc in range(4):
                c = c4 * 4 + cc
                g = c // 2
                V(lambda bi=bi, tmp=tmp, cc=cc, g=g: nc.vector.tensor_tensor(out=tmp[:, cc * 128:(cc + 1) * 128], in0=pb[bi][:, cc * 128:(cc + 1) * 128],
                                                                            in1=sc8[:, g * 128:(g + 1) * 128], op=ALU.add), [pb[bi], sc8], [tmp], loose=(cc > 0))
            V(lambda tmp=tmp, c4=c4, ugt=ugt, i=i: nc.vector.tensor_tensor(out=mixed2T[:, c4 * 4:(c4 + 1) * 4, i * 128:(i + 1) * 128],
                                                                         in0=tmp[:].rearrange("p (c t) -> p c t", t=128), in1=ugt[:, c4 * 4:(c4 + 1) * 4, :], op=ALU.mult),
              [tmp, ugt], [mixed2T], loose=True)

    out_proj(w_out_c, 1, y0_s, y0_s.t.ap(), y_out)


_NC = [None]


def _prep_core(c, I):
    b, j = c // 4, c % 4
    f32 = np.float32
    xp = I["x_prompt"][b].reshape(32, 128, D)
    own = list(range(8 * j, 8 * j + 8))
    others = [t for t in range(32) if t not in own]
    xall = np.zeros((35 * 128, D), f32)
    xall[0:3072] = xp[others].reshape(3072, D)
    xall[3072:4096] = xp[own].reshape(1024, D)
    xall[4096:4352] = I["x_sample"][4 * c:4 * c + 4].reshape(256, D)
    if j > 0:
        xall[4352:4368] = I["x_prompt"][b, 1024 * j - 16:1024 * j]
    cs = np.concatenate([I["c_prompt"][b:b + 1], I["c_sample"][4 * c:4 * c + 4]], axis=0)
    m = {}
    m["xall"] = xall
    m["csT"] = np.ascontiguousarray(cs.reshape(5, KC, 128).transpose(2, 1, 0).reshape(128, KC * 5))
    valid = np.array([1.0 if t < 8 * j else 0.0 for t in others] + [1.0] * 8, f32)
    m["meta_valid"] = np.ascontiguousarray(np.tile(valid[None, :], (128, 1)))
    m["meta_kneg"] = np.ascontiguousarray(np.tile(((valid - 1.0) * 30000.0)[None, :], (128, 1))).astype(f32)
    m["meta_prev"] = np.full((128, 1), 1.0 if j > 0 else 0.0, f32)
    corr = np.zeros((4, 16), f32)
    for gi in range(4):
        w = 2 ** (gi + 1)
        for t in range(16):
            corr[gi, t] = w / min(w, 1024 * j + t + 1)
    m["corr"] = np.ascontiguousarray(np.tile(corr.reshape(1, 64), (128, 1)))
    m["cache_k"] = np.ascontiguousarray(I["cache_k"][0, 4 * c:4 * c + 4].reshape(4 * 4096, 2048))
    m["cache_v"] = np.ascontiguousarray(I["cache_v"][0, 4 * c:4 * c + 4].reshape(4 * 4096, 2048))
    m["cache_lf"] = np.ascontiguousarray(I["cache_logf"][0, 4 * c:4 * c + 4].reshape(4 * 4096, NH))
    sp = I["state_pool"][0, 4 * c:4 * c + 4].reshape(4, 15, 16, 128).transpose(0, 2, 3, 1)
    spT = np.zeros((4, 16, 128, 16), f32)
    spT[..., 1:] = sp
    m["spT"] = spT
    return m


def _shared(I):
    f32 = np.float32
    m = {}
    m["w_ada"] = np.ascontiguousarray(I["w_ada"])
    m["b_adaT"] = np.ascontiguousarray(I["b_ada"].reshape(2, 96, 128).transpose(0, 2, 1))
    m["g_normT"] = np.ascontiguousarray(I["g_norm"].reshape(2, KC, 128).transpose(0, 2, 1))
    m["w_in_ab"] = np.ascontiguousarray(I["w_in_ab"][0])
    m["bf_bc"] = np.ascontiguousarray(np.tile(I["b_forget"][0][None, :], (128, 1)))
    m["gq_bc"] = np.ascontiguousarray(np.tile(I["g_q"][0][None, :], (128, 1)))
    m["gk_bc"] = np.ascontiguousarray(np.tile(I["g_k"][0][None, :], (128, 1)))
    m["w_pool"] = np.ascontiguousarray(I["w_pool"][0])
    m["ls_poolT"] = np.ascontiguousarray(I["ls_pool"][0].reshape(16, 128).T)
    m["w_out_ab"] = np.ascontiguousarray(I["w_out_ab"][0])
    m["w_in_c"] = np.ascontiguousarray(I["w_in_c"][0])
    m["gv_bc"] = np.ascontiguousarray(np.tile(I["g_v"][0][None, :], (128, 1)))
    m["bv_bc"] = np.ascontiguousarray(np.tile(I["b_v"][0][None, :], (128, 1)))
    ws = I["w_s"][0]
    wsT = np.zeros((2, 128, NH, 128), f32)
    wsT[0] = ws.transpose(2, 0, 1)
    blk = ws[:, :64, :64].transpose(2, 0, 1)
    wsT[1, 0:64, :, 0:64] = blk
    wsT[1, 64:128, :, 64:128] = blk
    m["wsT"] = wsT.reshape(2, 128, NH * 128)
    bs = I["b_s"][0]
    bs_bc = np.zeros((2, 128, NH, 128), f32)
    bs_bc[0] = bs[None, :, :]
    bs_bc[1, :, :, 0:64] = bs[None, :, 0:64]
    bs_bc[1, :, :, 64:128] = bs[None, :, 0:64]
    m["bs_bc"] = bs_bc.reshape(2, 128, NH * 128)
    m["w_out_c"] = np.ascontiguousarray(I["w_out_c"][0])
    r = np.arange(128)
    tri_inc = (r[:, None] <= r[None, :]).astype(f32)
    tri_rev = (r[:, None] > r[None, :]).astype(f32)
    tri_s = tri_inc * ((r[:, None] // 64) == (r[None, :] // 64)).astype(f32)
    m["consts"] = np.ascontiguousarray(np.concatenate([tri_inc, tri_rev, tri_s, np.ones((128, 128), f32)], axis=1))
    selc = np.zeros((128, NH, 128), f32)
    for h in range(NH):
        for base in (0, 32, 64):
            selc[base + h, h, :] = 1.0
    m["selc"] = selc.reshape(128, NH * 128)
    return m


def kernel(**inputs):
    I = {k: np.asarray(v) for k, v in inputs.items()}
    if _NC[0] is None:
        _NC[0] = build_nc()
    nc = _NC[0]
    shared = _shared(I)
    in_maps = []
    for c in range(8):
        m = dict(shared)
        m.update(_prep_core(c, I))
        in_maps.append(m)
    res = run_bass_kernel_spmd(nc, in_maps, core_ids=list(range(8)))
    R = res.results
    f32 = np.float32
    y_prompt = np.zeros((2, 4096, D), f32)
    y_sample = np.zeros((32, 64, D), f32)
    k_prompt = np.zeros((1, 2, 4096, NH, HD), f32)
    v_prompt = np.zeros((1, 2, 4096, NH, HD), f32)
    lf_prompt = np.zeros((1, 2, 4096, NH), f32)
    pool_prompt = np.zeros((1, 2, 15, 2048), f32)
    k_sample = np.zeros((1, 32, 64, NH, HD), f32)
    v_sample = np.zeros((1, 32, 64, NH, HD), f32)
    lf_sample = np.zeros((1, 32, 64, NH), f32)
    pool_sample = np.zeros((1, 32, 15, 2048), f32)
    sguv = np.zeros((1, 32, 64, D), f32)
    for c in range(8):
        b, j = c // 4, c % 4
        r = R[c]
        sl = slice(1024 * j, 1024 * (j + 1))
        ss = slice(4 * c, 4 * c + 4)
        y = np.asarray(r["y_out"])
        y_prompt[b, sl] = y[0:1024]
        y_sample[ss] = y[1024:1280].reshape(4, 64, D)
        ko = np.asarray(r["k_out"]); vo = np.asarray(r["v_out"]); lo = np.asarray(r["lf_out"])
        k_prompt[0, b, sl] = ko[0:1024].reshape(1024, NH, HD)
        v_prompt[0, b, sl] = vo[0:1024].reshape(1024, NH, HD)
        lf_prompt[0, b, sl] = lo[0:1024]
        k_sample[0, ss] = ko[1024:1280].reshape(4, 64, NH, HD)
        v_sample[0, ss] = vo[1024:1280].reshape(4, 64, NH, HD)
        lf_sample[0, ss] = lo[1024:1280].reshape(4, 64, NH)
        po = np.asarray(r["pool_out"])
        pt = po.transpose(0, 3, 1, 2).reshape(5, 15, 2048)
        if j == 3:
            pool_prompt[0, b] = pt[0]
        pool_sample[0, ss] = pt[1:5]
        sguv[0, ss] = np.asarray(r["sguv_out"]).reshape(4, 64, D)
    return (y_prompt, y_sample, k_prompt, v_prompt, lf_prompt, pool_prompt,
            k_sample, v_sample, lf_sample, pool_sample, sguv)
```
